# Optimizing a Trainium2 kernel written in Bass

```python
import math
import jax, jax.numpy as jnp
from jax import lax
import numpy as np

D_MODEL = 4096
BATCH = 4
SEQ = 4096
DEPTH = 1
DEC_BATCH = 1
DEC_SEQ = 16384
PAST_LEN = 128

W_A = D_MODEL // 2
DH_A = 64
H_A = W_A // (2 * DH_A)
W_B = D_MODEL // 2
N_B = 64
H_B = W_B // N_B
R_LORA = 128
C_SHIFT = 3 * W_B + 4 * R_LORA
N_IN = 3 * W_A + C_SHIFT + W_A + W_B + 2 * D_MODEL
Q_BLOCK = 128
ROPE_THETA = 10000.0
ATTN_SCALE = DH_A ** -0.5
NORM_EPS = 1e-6
SUBLN_EPS = 1e-5
LNX_EPS = 64e-5

kernel_name = "hybrid_diffattn_rwkv7_gated_encoder"


def rms_norm(x, g, eps=NORM_EPS):
    xf = x.astype(jnp.float32)
    y = xf * lax.rsqrt(jnp.mean(xf * xf, axis=-1, keepdims=True) + eps)
    return y.astype(x.dtype) * g


def lambda_init(layer_idx):
    return 0.8 - 0.6 * math.exp(-0.3 * layer_idx)


def rope(x):
    S = x.shape[1]
    half = DH_A // 2
    inv = 1.0 / (ROPE_THETA ** (jnp.arange(0, DH_A, 2, dtype=jnp.float32) / DH_A))
    ang = jnp.arange(S, dtype=jnp.float32)[:, None] * inv[None, :]
    cos = jnp.cos(ang)[None, :, None, None, :]
    sin = jnp.sin(ang)[None, :, None, None, :]
    xf = x.astype(jnp.float32)
    x1, x2 = xf[..., :half], xf[..., half:]
    return jnp.concatenate([x1 * cos - x2 * sin, x2 * cos + x1 * sin], axis=-1).astype(x.dtype)


def diff_attention(q, k, v, lam, subln_g, lam_init):
    B, S = q.shape[0], q.shape[1]
    nb = S // Q_BLOCK
    qb = q.reshape(B, nb, Q_BLOCK, H_A, 2, DH_A).swapaxes(0, 1)

    def block(qblk):
        s = jnp.einsum('bqhcd,bkhcd->bhcqk', qblk, k,
                       preferred_element_type=jnp.float32) * ATTN_SCALE
        pr = jax.nn.softmax(s, axis=-1)
        att = pr[:, :, 0] - lam * pr[:, :, 1]
        return jnp.einsum('bhqk,bkhe->bqhe', att.astype(v.dtype), v)

    o = lax.map(block, qb).swapaxes(0, 1).reshape(B, S, H_A, 2 * DH_A)
    o = rms_norm(o, subln_g, SUBLN_EPS) * (1.0 - lam_init)
    return o.reshape(B, S, W_A)


def centered_shift(p, mu_prev, mu_next):
    prev = jnp.pad(p[:, :-1], ((0, 0), (1, 0), (0, 0)))
    nxt = jnp.pad(p[:, 1:], ((0, 0), (0, 1), (0, 0)))
    return p + mu_prev * (prev - p) + mu_next * (nxt - p)


def wkv7_scan(r, w, k, v, kk, b, reverse):
    B, S = r.shape[0], r.shape[1]
    xs = tuple(t.astype(jnp.float32).reshape(B, S, H_B, N_B).swapaxes(0, 1)
               for t in (r, w, k, v, kk, b))

    def step(st, inp):
        r_t, w_t, k_t, v_t, kk_t, b_t = inp
        sa = jnp.einsum('bhij,bhj->bhi', st, -kk_t)
        st = st * w_t[:, :, None, :] + sa[..., None] * b_t[:, :, None, :] \
            + v_t[..., None] * k_t[:, :, None, :]
        y = jnp.einsum('bhij,bhj->bhi', st, r_t)
        return st, y

    st0 = jnp.zeros((B, H_B, N_B, N_B), jnp.float32)
    _, ys = lax.scan(step, st0, xs, reverse=reverse)
    return ys.swapaxes(0, 1)


def rwkv7_mix(slab, w0, w_lora, a0, a_lora, k_k, k_a, r_k, lnx_g, lnx_b):
    B, S = slab.shape[0], slab.shape[1]
    r = slab[..., :W_B]
    k = slab[..., W_B:2 * W_B]
    v = slab[..., 2 * W_B:3 * W_B]
    lo = slab[..., 3 * W_B:].reshape(B, S, 4, R_LORA)
    pw, pa = lo[:, :, 0:2], lo[:, :, 2:4]
    wl = (w0 + jnp.einsum('bsdr,drc->bsdc', jnp.tanh(pw), w_lora)).astype(jnp.float32)
    decay = jnp.exp(-jnp.exp(-jax.nn.softplus(-wl) - 0.5))
    a = jax.nn.sigmoid((a0 + jnp.einsum('bsdr,drc->bsdc', pa, a_lora)).astype(jnp.float32))
    kkh = (k * k_k).astype(jnp.float32).reshape(B, S, H_B, N_B)
    kkh = kkh / jnp.maximum(jnp.sqrt(jnp.sum(kkh * kkh, axis=-1, keepdims=True)), 1e-12)
    kk = kkh.reshape(B, S, W_B)
    kdir = k[:, :, None, :].astype(jnp.float32) * (1.0 + (a - 1.0) * k_a)
    bdir = kk[:, :, None, :] * a
    y = wkv7_scan(r, decay[:, :, 0], kdir[:, :, 0], v, kk, bdir[:, :, 0], reverse=False) \
        + wkv7_scan(r, decay[:, :, 1], kdir[:, :, 1], v, kk, bdir[:, :, 1], reverse=True)
    mu = jnp.mean(y, axis=-1, keepdims=True)
    var = jnp.mean(jnp.square(y - mu), axis=-1, keepdims=True)
    y = ((y - mu) * lax.rsqrt(var + LNX_EPS)).reshape(B, S, W_B) * lnx_g + lnx_b
    rh = r.astype(jnp.float32).reshape(B, S, 1, H_B, N_B)
    bonus_w = jnp.sum(rh * kdir.reshape(B, S, 2, H_B, N_B) * r_k, axis=(2, 4))
    bonus = bonus_w[..., None] * v.astype(jnp.float32).reshape(B, S, H_B, N_B)
    return (y + bonus.reshape(B, S, W_B)).astype(slab.dtype)


def mixer_layer(x, lam_init, norm_g, w_in, mu_prev, mu_next, lam_q1, lam_k1, lam_q2, lam_k2,
                subln_g, w0, w_lora, a0, a_lora, k_k, k_a, r_k, lnx_g, lnx_b, w_oA, w_oB, w_out):
    B, S, _ = x.shape
    h = rms_norm(x, norm_g)
    p = jnp.einsum('bsd,dn->bsn', h, w_in)
    q = p[..., 0:W_A].reshape(B, S, H_A, 2, DH_A)
    k = p[..., W_A:2 * W_A].reshape(B, S, H_A, 2, DH_A)
    v = p[..., 2 * W_A:3 * W_A].reshape(B, S, H_A, 2 * DH_A)
    slab = centered_shift(p[..., 3 * W_A:3 * W_A + C_SHIFT], mu_prev, mu_next)
    off = 3 * W_A + C_SHIFT
    z_a = p[..., off:off + W_A]
    z_b = p[..., off + W_A:off + W_A + W_B]
    off2 = off + W_A + W_B
    g_a = p[..., off2:off2 + D_MODEL]
    g_b = p[..., off2 + D_MODEL:off2 + 2 * D_MODEL]

    lam = (jnp.exp(jnp.sum(lam_q1.astype(jnp.float32) * lam_k1.astype(jnp.float32)))
           - jnp.exp(jnp.sum(lam_q2.astype(jnp.float32) * lam_k2.astype(jnp.float32)))
           + lam_init)
    y_a = diff_attention(rope(q), rope(k), v, lam, subln_g, lam_init)
    y_b = rwkv7_mix(slab, w0, w_lora, a0, a_lora, k_k, k_a, r_k, lnx_g, lnx_b)

    o_a = jnp.einsum('bsc,cd->bsd', y_a * jax.nn.silu(z_a), w_oA)
    o_b = jnp.einsum('bsc,cd->bsd', y_b * jax.nn.silu(z_b), w_oB)
    m = jax.nn.sigmoid(g_a) * o_a + jax.nn.sigmoid(g_b) * o_b
    return x + jnp.einsum('bsd,de->bse', m, w_out)


def setup_inputs(seed: int = 0) -> dict:
    key = jax.random.key(seed)
    ks = jax.random.split(key, 24)
    f32 = jnp.float32

    def nrm(k, shape, s):
        return jax.random.normal(k, shape, f32) * s

    return {
        "x_prompt": nrm(ks[0], (BATCH, SEQ, D_MODEL), 1.0),
        "x_sample": nrm(ks[1], (DEC_BATCH, DEC_SEQ, D_MODEL), 1.0),
        "norm_g": 1.0 + nrm(ks[2], (DEPTH, D_MODEL), 0.05),
        "w_in": nrm(ks[3], (DEPTH, D_MODEL, N_IN), D_MODEL ** -0.5),
        "mu_prev": jax.random.uniform(ks[4], (DEPTH, C_SHIFT), f32, 0.0, 0.5),
        "mu_next": jax.random.uniform(ks[5], (DEPTH, C_SHIFT), f32, 0.0, 0.5),
        "lam_q1": nrm(ks[6], (DEPTH, DH_A), 0.1),
        "lam_k1": nrm(ks[7], (DEPTH, DH_A), 0.1),
        "lam_q2": nrm(ks[8], (DEPTH, DH_A), 0.1),
        "lam_k2": nrm(ks[9], (DEPTH, DH_A), 0.1),
        "subln_g": 1.0 + nrm(ks[10], (DEPTH, 2 * DH_A), 0.05),
        "w0": jax.random.uniform(ks[11], (DEPTH, 2, W_B), f32, -4.0, 0.0),
        "w_lora": nrm(ks[12], (DEPTH, 2, R_LORA, W_B), 0.5 * R_LORA ** -0.5),
        "a0": nrm(ks[13], (DEPTH, 2, W_B), 0.5),
        "a_lora": nrm(ks[14], (DEPTH, 2, R_LORA, W_B), 0.5 * R_LORA ** -0.5),
        "k_k": 0.85 + nrm(ks[15], (DEPTH, W_B), 0.05),
        "k_a": 1.0 + nrm(ks[16], (DEPTH, W_B), 0.05),
        "r_k": nrm(ks[17], (DEPTH, H_B, N_B), 0.1),
        "lnx_g": 1.0 + nrm(ks[18], (DEPTH, W_B), 0.05),
        "lnx_b": nrm(ks[19], (DEPTH, W_B), 0.02),
        "w_oA": nrm(ks[20], (DEPTH, W_A, D_MODEL), W_A ** -0.5),
        "w_oB": nrm(ks[21], (DEPTH, W_B, D_MODEL), W_B ** -0.5),
        "w_out": nrm(ks[22], (DEPTH, D_MODEL, D_MODEL), D_MODEL ** -0.5),
        "final_g": 1.0 + nrm(ks[23], (D_MODEL,), 0.05),
    }


def reference(x_prompt, x_sample, norm_g, w_in, mu_prev, mu_next, lam_q1, lam_k1, lam_q2, lam_k2,
              subln_g, w0, w_lora, a0, a_lora, k_k, k_a, r_k, lnx_g, lnx_b, w_oA, w_oB, w_out,
              final_g):
    def trunk(x):
        for l in range(DEPTH):
            x = mixer_layer(x, lambda_init(l), norm_g[l], w_in[l], mu_prev[l], mu_next[l],
                            lam_q1[l], lam_k1[l], lam_q2[l], lam_k2[l], subln_g[l],
                            w0[l], w_lora[l], a0[l], a_lora[l], k_k[l], k_a[l], r_k[l],
                            lnx_g[l], lnx_b[l], w_oA[l], w_oB[l], w_out[l])
        return rms_norm(x, final_g)

    y_prompt = trunk(x_prompt)
    y_sample = trunk(x_sample)
    return (y_prompt, y_sample)
```

```python
import math
from contextlib import ExitStack
import numpy as np
import concourse.bass as bass
import concourse.mybir as mybir
from concourse.bass_utils import run_bass_kernel_spmd

F32 = mybir.dt.float32
BF16 = mybir.dt.bfloat16
ALU = mybir.AluOpType
AF = mybir.ActivationFunctionType
AX = mybir.AxisListType

ENGS = ["tensor", "vector", "scalar", "gpsimd", "sync"]
NCORES = 8
D = 4096
P = 128
NDC = D // P
NORM_EPS = 1e-6
NPOOL = 20


class Buf:
    def __init__(self, name, t=None):
        self.name = name
        self.t = t
        self.writers = []
        self.readers = []
        self.base = None
        self.basew = []
        self.open = False
        self.sem = None

    def __getitem__(self, idx):
        return self.t[idx]


class Lazy:
    def __init__(self, f):
        self.f = f


class MK:
    def __init__(self, nc):
        self.nc = nc
        self.stack = ExitStack()
        self.sems = []
        self.semcount = []
        self.esem = {}
        for e in ENGS:
            self.esem[e] = self._newsem("es_" + e)
        self.pool = [self._newsem("dp%d" % i) for i in range(NPOOL)]
        self.waited = {e: {} for e in ENGS}
        self.streams = None
        self.pstack = None
        self.ninst = 0
        self._dma_final = []
        self.pool_next = 0
        self.pbufs = []

    def _newsem(self, name):
        s = self.stack.enter_context(self.nc.semaphore(name))
        self.sems.append(s)
        self.semcount.append(0)
        return len(self.sems) - 1

    def begin(self, name):
        self.pname = name
        self.pstack = ExitStack()
        self.streams = {e: [] for e in ENGS}
        self._dma_final = []
        self.pool_next = 0
        self.pbufs = []

    def sb(self, name, shape, dtype):
        t = self.pstack.enter_context(self.nc.sbuf_tensor(self.pname + "_" + name, list(shape), dtype))
        return Buf(name, t)

    def ps(self, name, shape, dtype=F32):
        nb = int(np.prod(shape[1:])) * (4 if dtype == F32 else 2)
        assert nb == 2048, "PSUM bufs must be exactly one bank"
        t = self.pstack.enter_context(self.nc.psum_tensor(self.pname + "_" + name, list(shape), dtype))
        return Buf(name, t)

    def dsem(self, buf):
        if buf.sem is None:
            assert self.pool_next < NPOOL, "out of DMA semaphores in phase " + self.pname
            buf.sem = self.pool[self.pool_next]
            self.pool_next += 1
            self.pbufs.append(buf)
        return buf.sem

    def end(self):
        fin = {}
        for (eng, sem, val) in self._dma_final:
            k = (eng, sem)
            if k not in fin or fin[k] < val:
                fin[k] = val
        for (eng, sem), val in fin.items():
            self._wait(eng, sem, val)
        streams = self.streams
        sems = self.sems
        with self.nc.Block() as block:
            for e in ENGS:
                ops = streams[e]
                if not ops:
                    continue

                def body(h, ops=ops):
                    for o in ops:
                        if o[0] == "w":
                            h.wait_ge(sems[o[1]], o[2])
                        elif o[0] == "i":
                            kw = {k: (v.f(h) if isinstance(v, Lazy) else v) for k, v in o[3].items()}
                            if o[2] is not None:
                                getattr(h, o[1])(*o[2], **kw).then_inc(sems[o[4]], o[5])
                            else:
                                getattr(h, o[1])(**kw).then_inc(sems[o[4]], o[5])
                getattr(block, e)(body)
        self.pstack.close()
        self.pstack = None
        self.streams = None
        self._dma_final = []
        for b in self.pbufs:
            b.sem = None
            b.writers = []
            b.readers = []
            b.base = None
            b.open = False
        self.pbufs = []

    def _wait(self, eng, sem, val):
        w = self.waited[eng]
        if w.get(sem, 0) >= val:
            return
        w[sem] = val
        self.streams[eng].append(("w", sem, val))

    def _deps(self, eng, r, w, join):
        hard = []
        soft = []
        for b in r:
            hard += b.writers
            if b.base is not None:
                hard += b.base
            b.open = False
        for b in w:
            if join:
                if not b.open:
                    b.base = b.writers + b.readers
                    b.basew = list(b.writers)
                    b.writers = []
                    b.readers = []
                    b.open = True
                hard += b.basew
                soft += b.base
            else:
                hard += b.writers
                soft += b.readers
                if b.base is not None:
                    soft += b.base
                b.open = False
        for (deng, sem, val) in hard:
            if deng == eng and eng == "tensor":
                continue
            self._wait(eng, sem, val)
        for (deng, sem, val) in soft:
            if deng == eng:
                continue
            self._wait(eng, sem, val)

    def _commit(self, dep, r, w, join):
        for b in r:
            b.readers.append(dep)
            if len(b.readers) > 48:
                b.readers = b.readers[-48:]
        for b in w:
            if join:
                b.writers.append(dep)
            else:
                b.writers = [dep]
                b.readers = []
                b.base = None

    def op(self, eng, method, r=(), w=(), join=False, args=None, **kw):
        r = [b for b in r if b is not None]
        w = [b for b in w if b is not None]
        self._deps(eng, r, w, join)
        sem = self.esem[eng]
        self.semcount[sem] += 1
        val = self.semcount[sem]
        self.streams[eng].append(("i", method, args, kw, sem, 1))
        self._commit((eng, sem, val), r, w, join)
        self.ninst += 1

    def dma(self, eng, out, in_, r=(), w=(), chan=None, join=False, **kw):
        r = [b for b in r if b is not None]
        w = [b for b in w if b is not None]
        if chan is None:
            chan = (w + r)[0]
        self._deps(eng, r, w, join)
        sem = self.dsem(chan)
        self.semcount[sem] += 16
        val = self.semcount[sem]
        kw = dict(kw)
        kw["out"] = out
        kw["in_"] = in_
        self.streams[eng].append(("i", "dma_start", None, kw, sem, 16))
        self._commit(("dma", sem, val), r, w, join)
        self._dma_final.append((eng, sem, val))
        self.ninst += 1

    def allgather(self, in_buf, out_buf):
        eng = "gpsimd"
        self._deps(eng, [in_buf], [out_buf], False)
        sem = self.dsem(out_buf)
        self.semcount[sem] += 1
        val = self.semcount[sem]
        kw = dict(replica_groups=[list(range(NCORES))], ins=[in_buf.t.ap().opt()], outs=[out_buf.t.ap().opt()])
        self.streams[eng].append(("i", "collective_compute", ("AllGather", ALU.bypass), kw, sem, 1))
        self._commit(("dma", sem, val), [in_buf], [out_buf], False)
        self._dma_final.append((eng, sem, val))


class CFG:
    def __init__(self, batch=4, seq=4096, dec_seq=16384):
        self.batch, self.seq, self.dec_seq = batch, seq, dec_seq
        self.ntok = batch * seq + dec_seq
        assert self.ntok % (NCORES * P) == 0
        self.tpc = self.ntok // NCORES
        self.seqs = [(b * seq, seq) for b in range(batch)] + [(batch * seq, dec_seq)]


W_A = D // 2
W_B = D // 2
C_SHIFT = 3 * W_B + 4 * 128
OFF_Z = 3 * W_A + C_SHIFT
OFF_G = OFF_Z + W_A + W_B
LAM_INIT = 0.8 - 0.6 * math.exp(-0.3 * 0)
SUBLN_EPS = 1e-5
LNX_EPS = 64e-5
ATTN_SCALE = 64 ** -0.5


def build(cfg, debug=None):
    nc = bass.Bass("TRN2", target_bir_lowering=False)
    TPC = cfg.tpc
    NT = TPC // P
    NTOK = cfg.ntok
    SMAX = max(s for _, s in cfg.seqs)
    dt = {}

    def din(name, shape, dtype=F32):
        dt[name] = nc.dram_tensor(name, list(shape), dtype, kind="ExternalInput")
        return dt[name]

    def dscr(name, shape, dtype):
        return Buf(name, nc.dram_tensor(name, list(shape), dtype))

    INSPEC = {"x_own": ("x_own", [TPC, D]), "norm_g": ("norm_g", [1, D]), "ident_in": ("ident", [P, P]), "w_inA": ("w_inA", [D, 1024]), "cos_t": ("cos_t", [SMAX, 32]), "sin_t": ("sin_t", [SMAX, 32]), "lam_in": ("lam_in", [4, 64]), "subln_g": ("subln_g", [1, 128]), "w_inB": ("w_inB", [D, 1536]), "mu_in": ("mu_in", [P, 2, 10]), "wlora_in": ("wlora_in", [4, 128, 256]), "rowp_in": ("rowp_in", [9, 256]), "masks_in": ("masks", [4, P, P])}

    def I(key):
        nm, shp = INSPEC[key]
        if nm not in dt:
            dt[nm] = nc.dram_tensor(nm, list(shp), F32, kind="ExternalInput")
        return dt[nm]

    hT_loc = dscr("hT_loc", [D, TPC], BF16)
    hT_all = dscr("hT_all", [NCORES * D, TPC], BF16)
    QT = [dscr("QT%d" % h, [P, SMAX], BF16) for h in range(2)]
    KT = [dscr("KT%d" % h, [P, SMAX], BF16) for h in range(2)]
    VS = dscr("VS", [SMAX, 256], BF16)
    ZA = dscr("ZA", [SMAX, 256], F32)
    ygT_loc = dscr("ygT_loc", [512, NTOK], BF16)
    dbg = None
    if debug == "p1":
        dbg = nc.dram_tensor("dbg", [NCORES * D, TPC], BF16, kind="ExternalOutput")
    if debug in ("p2a", "p2b"):
        dbg = nc.dram_tensor("dbg", [512, NTOK], BF16, kind="ExternalOutput")
    if debug == "qkv":
        dbg = nc.dram_tensor("dbg", [256 + SMAX, SMAX], BF16, kind="ExternalOutput")

    K = MK(nc)

    def V(m, r, w, **kw):
        K.op("vector", m, r, w, **kw)

    def S(m, r, w, **kw):
        K.op("scalar", m, r, w, **kw)

    def T(m, r, w, **kw):
        K.op("tensor", m, r, w, **kw)

    def evac(i, out, in_, r, w):
        if i % 2 == 0:
            S("copy", r, w, out=out, in_=in_)
        else:
            V("tensor_copy", r, w, out=out, in_=in_)

    def load_ident(dtype=BF16):
        identb = K.sb("identb", [P, P], dtype)
        K.dma("gpsimd", identb[:, :], I("ident_in").ap()[:, :], w=[identb])
        return identb

    def phase1():
        K.begin("p1")
        gbc = K.sb("gbc", [P, D], F32)
        K.dma("gpsimd", gbc[:, :], I("norm_g").ap()[0:1, :].partition_broadcast(P), w=[gbc])
        identb = load_ident()
        xt = [K.sb("xt%d" % i, [P, D], F32) for i in range(2)]
        junk = K.sb("junk", [P, D], BF16)
        hb = [K.sb("hb%d" % i, [P, D], BF16) for i in range(2)]
        ss = [K.sb("ss%d" % i, [P, 1], F32) for i in range(2)]
        rs = [K.sb("rs%d" % i, [P, 1], F32) for i in range(2)]
        hTt = [K.sb("hTt%d" % i, [P, NDC, P], BF16) for i in range(2)]
        tps = [K.ps("tps%d" % i, [P, 8, P], BF16) for i in range(4)]
        hT_loc_v = hT_loc.t.ap().rearrange("(c p) t -> p c t", p=P)
        for i in range(NT):
            s = i % 2
            K.dma("sync", xt[s][:, :], I("x_own").ap()[i * P:(i + 1) * P, :], w=[xt[s]])
            S("activation", [xt[s]], [junk, ss[s]], out=junk[:, :], in_=xt[s][:, :], func=AF.Square, accum_out=ss[s][:, :])
            S("activation", [ss[s]], [rs[s]], out=rs[s][:, :], in_=ss[s][:, :], func=AF.Sqrt, scale=1.0 / D, bias=NORM_EPS)
            V("reciprocal", [rs[s]], [rs[s]], out=rs[s][:, :], in_=rs[s][:, :])
            V("scalar_tensor_tensor", [xt[s], rs[s], gbc], [hb[s]], out=hb[s][:, :], in0=xt[s][:, :],
              scalar=rs[s][:, 0:1], in1=gbc[:, :], op0=ALU.mult, op1=ALU.mult)
            for g in range(NDC // 8):
                pb = tps[g % 4]
                for j in range(8):
                    c = g * 8 + j
                    T("transpose", [hb[s], identb], [pb], out=pb[:, j, :], in_=hb[s][:, c * P:(c + 1) * P],
                      identity=identb[:, :])
                evac(g, hTt[s][:, g * 8:(g + 1) * 8, :], pb[:, :, :], [pb], [hTt[s]])
            K.dma("gpsimd", hT_loc_v[:, :, i * P:(i + 1) * P], hTt[s][:, :, :], r=[hTt[s]], w=[hT_loc],
                  chan=hTt[s], join=True)
        K.end()
        K.begin("ag1")
        K.allgather(hT_loc, hT_all)
        if debug == "p1":
            K.dma("gpsimd", dbg.ap()[:, :], hT_all.t.ap()[:, :], r=[hT_all], w=[], chan=hT_all)
        K.end()

    def hT_src(g0, n):
        rank = g0 // TPC
        t0 = g0 % TPC
        assert t0 + n <= TPC
        v = hT_all.t.ap()[rank * D:(rank + 1) * D, t0:t0 + n]
        return v.rearrange("(c p) t -> p c t", p=P)

    def phase2a_proj(g0, S_):
        K.begin("pa%d" % g0)
        TBL = min(512, TPC)
        wA = K.sb("wA", [P, NDC, 1024], BF16)
        w_v = I("w_inA").ap().rearrange("(c p) n -> p c n", p=P)
        for q in range(4):
            K.dma("gpsimd", wA[:, q * 8:(q + 1) * 8, :], w_v[:, q * 8:(q + 1) * 8, :], w=[wA], join=True)
        identb = load_ident()
        hTb = [K.sb("hTb%d" % i, [P, NDC, TBL], BF16) for i in range(2)]
        cs = [K.sb("cs%d" % i, [P, 32], F32) for i in range(2)]
        sn = [K.sb("sn%d" % i, [P, 32], F32) for i in range(2)]
        pq = [K.ps("pq%d" % i, [P, 512], F32) for i in range(2)]
        pv = [K.ps("pv%d" % i, [P, 512], F32) for i in range(2)]
        ptr = [K.ps("ptr%d" % i, [P, 8, P], BF16) for i in range(2)]
        ta = K.sb("ta", [P, 8, 32], F32)
        tb = K.sb("tb", [P, 8, 32], F32)
        qk = [K.sb("qk%d" % i, [P, 512], BF16) for i in range(2)]
        qkT = [K.sb("qkT%d" % i, [P, 4, TBL], BF16) for i in range(2)]
        vb = [K.sb("vb%d" % i, [P, 256], BF16) for i in range(2)]
        zb = [K.sb("zb%d" % i, [P, 256], F32) for i in range(2)]
        nblk = S_ // TBL
        for bi in range(nblk):
            hs = hTb[bi % 2]
            qs_ = qkT[bi % 2]
            for q in range(4):
                K.dma("sync", hs[:, q * 8:(q + 1) * 8, :], hT_src(g0 + bi * TBL, TBL)[:, q * 8:(q + 1) * 8, :],
                      r=[hT_all], w=[hs], join=True)
            for ti in range(TBL // P):
                it = bi * (TBL // P) + ti
                s = it % 2
                pos0 = bi * TBL + ti * P
                K.dma("sync", cs[s][:, :], I("cos_t").ap()[pos0:pos0 + P, :], w=[cs[s]])
                K.dma("sync", sn[s][:, :], I("sin_t").ap()[pos0:pos0 + P, :], w=[sn[s]])
                for c in range(NDC):
                    for bk, pp in ((0, pq[s]), (1, pv[s])):
                        T("matmul", [hs, wA], [pp], out=pp[:, :], lhsT=hs[:, c, ti * P:(ti + 1) * P],
                          rhs=wA[:, c, bk * 512:(bk + 1) * 512], start=(c == 0), stop=(c == NDC - 1))
                pqv = pq[s][:, :].rearrange("p (b t d) -> p b t d", b=8, t=2)
                x1 = pqv[:, :, 0, :]
                x2 = pqv[:, :, 1, :]
                cb = cs[s][:, None, :].broadcast_to([P, 8, 32])
                sb_ = sn[s][:, None, :].broadcast_to([P, 8, 32])
                qkv = qk[s][:, :].rearrange("p (b t d) -> p b t d", b=8, t=2)
                V("tensor_tensor", [pq[s], cs[s]], [ta], out=ta[:, :, :], in0=x1, in1=cb, op=ALU.mult)
                V("tensor_tensor", [pq[s], sn[s]], [tb], out=tb[:, :, :], in0=x2, in1=sb_, op=ALU.mult)
                V("tensor_tensor", [ta, tb], [qk[s]], out=qkv[:, :, 0, :], in0=ta[:, :, :], in1=tb[:, :, :], op=ALU.subtract)
                V("tensor_tensor", [pq[s], cs[s]], [ta], out=ta[:, :, :], in0=x2, in1=cb, op=ALU.mult)
                V("tensor_tensor", [pq[s], sn[s]], [tb], out=tb[:, :, :], in0=x1, in1=sb_, op=ALU.mult)
                V("tensor_tensor", [ta, tb], [qk[s]], out=qkv[:, :, 1, :], in0=ta[:, :, :], in1=tb[:, :, :], op=ALU.add)
                for j in range(4):
                    T("transpose", [qk[s], identb], [ptr[s]], out=ptr[s][:, j, :], in_=qk[s][:, j * P:(j + 1) * P],
                      identity=identb[:, :])
                S("copy", [ptr[s]], [qs_], out=qs_[:, :, ti * P:(ti + 1) * P], in_=ptr[s][:, 0:4, :])
                S("copy", [pv[s]], [vb[s]], out=vb[s][:, :], in_=pv[s][:, 0:256])
                S("activation", [pv[s]], [zb[s]], out=zb[s][:, :], in_=pv[s][:, 256:512], func=AF.Silu)
                K.dma("sync", VS.t.ap()[pos0:pos0 + P, :], vb[s][:, :], r=[vb[s]], w=[], chan=vb[s])
                K.dma("sync", ZA.t.ap()[pos0:pos0 + P, :], zb[s][:, :], r=[zb[s]], w=[], chan=zb[s])
            t0 = bi * TBL
            for hh in range(2):
                K.dma("gpsimd", QT[hh].t.ap()[:, t0:t0 + TBL], qs_[:, hh, :], r=[qs_], w=[], chan=qs_)
                K.dma("gpsimd", KT[hh].t.ap()[:, t0:t0 + TBL], qs_[:, 2 + hh, :], r=[qs_], w=[], chan=qs_)
        K.end()

    def phase2a_attn(g0, S_):
        K.begin("at%d" % g0)
        QB = min(512, S_)
        NQS = QB // P
        NKC = S_ // P
        identb = load_ident()
        lamv = K.sb("lamv", [P, 4, 64], F32)
        for a in range(4):
            K.dma("gpsimd", lamv[:, a, :], I("lam_in").ap()[a:a + 1, :].partition_broadcast(P), w=[lamv], join=True)
        lt = K.sb("lt", [P, 2, 64], F32)
        l2 = K.sb("l2", [P, 2], F32)
        nlam = K.sb("nlam", [P, 1], F32)
        V("tensor_tensor", [lamv], [lt], out=lt[:, 0, :], in0=lamv[:, 0, :], in1=lamv[:, 1, :], op=ALU.mult)
        V("tensor_tensor", [lamv], [lt], out=lt[:, 1, :], in0=lamv[:, 2, :], in1=lamv[:, 3, :], op=ALU.mult)
        V("tensor_reduce", [lt], [l2], out=l2[:, :], in_=lt[:, :, :], axis=AX.X, op=ALU.add)
        S("activation", [l2], [l2], out=l2[:, :], in_=l2[:, :], func=AF.Exp)
        V("tensor_tensor", [l2], [nlam], out=nlam[:, :], in0=l2[:, 1:2], in1=l2[:, 0:1], op=ALU.subtract)
        V("tensor_scalar", [nlam], [nlam], out=nlam[:, :], in0=nlam[:, :], scalar1=-LAM_INIT, scalar2=None, op0=ALU.add)
        sgb = K.sb("sgb", [P, 128], F32)
        K.dma("gpsimd", sgb[:, :], I("subln_g").ap()[0:1, :].partition_broadcast(P), w=[sgb])
        qT = K.sb("qT", [P, S_], BF16)
        kT = K.sb("kT", [P, S_], BF16)
        vS = K.sb("vS", [P, NKC, 129], BF16)
        V("memset", [], [vS], args=(vS[:, :, 128:129], 1.0))
        stp = [K.ps("stp%d" % i, [P, 512], F32) for i in range(4)]
        accb = [K.ps("acc%d" % i, [P, 512], F32) for i in range(3)]
        ptr = K.ps("ptr", [P, 8, P], BF16)
        NPT = 4
        pT = [K.sb("pT%d" % i, [P, 512], BF16) for i in range(NPT)]

        def acc(m, qs):
            i = m * 4 + qs
            return accb[i // 3], (i % 3) * 129

        rl = K.sb("rl", [P, 8], F32)
        t1 = [K.sb("t1_%d" % i, [P, 128], F32) for i in range(2)]
        yy = [K.sb("yy_%d" % i, [P, 128], F32) for i in range(2)]
        jk = K.sb("jk", [P, 128], F32)
        ms = [K.sb("ms_%d" % i, [P, 1], F32) for i in range(2)]
        za = [K.sb("za%d" % i, [P, NQS, 128], F32) for i in range(2)]
        yg = [K.sb("yg%d" % i, [P, 128], BF16) for i in range(2)]
        ygT = [K.sb("ygT%d" % i, [P, QB], BF16) for i in range(2)]
        for hh in range(2):
            K.dma("sync", qT[:, :], QT[hh].t.ap()[:, 0:S_], w=[qT])
            K.dma("sync", kT[:, :], KT[hh].t.ap()[:, 0:S_], w=[kT])
            vsrc = VS.t.ap()[0:S_, hh * 128:(hh + 1) * 128].rearrange("(n p) e -> p n e", p=P)
            for n0 in range(0, NKC, 16):
                n1 = min(NKC, n0 + 16)
                K.dma("gpsimd", vS[:, n0:n1, 0:128], vsrc[:, n0:n1, :], w=[vS], join=True)
            for qb in range(S_ // QB):
                zs = za[qb % 2]
                K.dma("sync", zs[:, :, :],
                      ZA.t.ap()[qb * QB:(qb + 1) * QB, hh * 128:(hh + 1) * 128].rearrange("(n p) e -> p n e", p=P),
                      w=[zs])
                for a in accb:
                    V("memset", [], [a], args=(a[:, :], 0.0))
                steps = [(kc, m) for kc in range(NKC) for m in range(2)]

                def emit_st(i):
                    kc, m = steps[i]
                    sp = stp[i % 4]
                    T("matmul", [kT, qT], [sp], out=sp[:, 0:QB], lhsT=kT[m * 64:(m + 1) * 64, kc * P:(kc + 1) * P],
                      rhs=qT[m * 64:(m + 1) * 64, qb * QB:(qb + 1) * QB], start=True, stop=True)
                    pt = pT[i % NPT]
                    S("activation", [sp], [pt], out=pt[:, 0:QB], in_=sp[:, 0:QB], func=AF.Exp, scale=ATTN_SCALE)

                def emit_pv(i):
                    kc, m = steps[i]
                    pt = pT[i % NPT]
                    for qs in range(NQS):
                        ab, off = acc(m, qs)
                        T("matmul", [pt, vS], [ab], out=ab[:, off:off + 129], lhsT=pt[:, qs * P:(qs + 1) * P],
                          rhs=vS[:, kc, :], start=False, stop=False, skip_group_check=True)
                LA = 2
                for i in range(min(LA, len(steps))):
                    emit_st(i)
                for i in range(len(steps)):
                    if i + LA < len(steps):
                        emit_st(i + LA)
                    emit_pv(i)
                ygs = ygT[qb % 2]
                for m in range(2):
                    for qs in range(NQS):
                        ab, off = acc(m, qs)
                        V("reciprocal", [ab], [rl], out=rl[:, m * 4 + qs:m * 4 + qs + 1], in_=ab[:, off + 128:off + 129])
                V("tensor_scalar", [rl, nlam], [rl], out=rl[:, 4:8], in0=rl[:, 4:8], scalar1=nlam[:, 0:1], scalar2=None,
                  op0=ALU.mult)
                for qs in range(NQS):
                    u = qs % 2
                    a1, o1 = acc(0, qs)
                    a2, o2 = acc(1, qs)
                    V("tensor_scalar", [a1, rl], [t1[u]], out=t1[u][:, :], in0=a1[:, o1:o1 + 128], scalar1=rl[:, qs:qs + 1],
                      scalar2=None, op0=ALU.mult)
                    V("scalar_tensor_tensor", [a2, rl, t1[u]], [yy[u]], out=yy[u][:, :], in0=a2[:, o2:o2 + 128],
                      scalar=rl[:, 4 + qs:5 + qs], in1=t1[u][:, :], op0=ALU.mult, op1=ALU.add)
                    S("activation", [yy[u]], [jk, ms[u]], out=jk[:, :], in_=yy[u][:, :], func=AF.Square, accum_out=ms[u][:, :])
                    S("activation", [ms[u]], [ms[u]], out=ms[u][:, :], in_=ms[u][:, :], func=AF.Sqrt, scale=1.0 / 128,
                      bias=SUBLN_EPS)
                    V("reciprocal", [ms[u]], [ms[u]], out=ms[u][:, :], in_=ms[u][:, :])
                    V("tensor_scalar", [ms[u]], [ms[u]], out=ms[u][:, :], in0=ms[u][:, :], scalar1=1.0 - LAM_INIT,
                      scalar2=None, op0=ALU.mult)
                    V("scalar_tensor_tensor", [yy[u], ms[u], sgb], [yy[u]], out=yy[u][:, :], in0=yy[u][:, :],
                      scalar=ms[u][:, 0:1], in1=sgb[:, :], op0=ALU.mult, op1=ALU.mult)
                    V("tensor_tensor", [yy[u], zs], [yg[u]], out=yg[u][:, :], in0=yy[u][:, :], in1=zs[:, qs, :], op=ALU.mult)
                    T("transpose", [yg[u], identb], [ptr], out=ptr[:, qs, :], in_=yg[u][:, :], identity=identb[:, :])
                S("copy", [ptr], [ygs], out=ygs[:, :].rearrange("p (a t) -> p a t", a=NQS), in_=ptr[:, 0:NQS, :])
                K.dma("gpsimd", ygT_loc.t.ap()[hh * 128:(hh + 1) * 128, g0 + qb * QB:g0 + (qb + 1) * QB], ygs[:, :],
                      r=[ygs], w=[], chan=ygs)
        K.end()

    PT = dscr("PT", [1536, SMAX], F32)
    PREP = dscr("PREP", [SMAX, 10, 256], F32)
    YD = [dscr("YD%d" % d, [SMAX, 256], F32) for d in range(2)]
    DECAY_C = math.exp(-0.5)

    def phase2b_proj(g0, S_):
        K.begin("pb%d" % g0)
        TBL = min(512, TPC)
        wB = K.sb("wB", [P, NDC, 1536], BF16)
        w_v = I("w_inB").ap().rearrange("(c p) n -> p c n", p=P)
        for q in range(8):
            K.dma("gpsimd", wB[:, q * 4:(q + 1) * 4, :], w_v[:, q * 4:(q + 1) * 4, :], w=[wB], join=True)
        hTb = [K.sb("hTb%d" % i, [P, NDC, TBL], BF16) for i in range(2)]
        pp = [K.ps("pp%d" % i, [P, 512], F32) for i in range(4)]
        ob = [K.sb("ob%d" % i, [P, TBL], F32) for i in range(4)]
        for bi in range(S_ // TBL):
            hs = hTb[bi % 2]
            for q in range(4):
                K.dma("sync", hs[:, q * 8:(q + 1) * 8, :], hT_src(g0 + bi * TBL, TBL)[:, q * 8:(q + 1) * 8, :],
                      r=[hT_all], w=[hs], join=True)
            for ct in range(12):
                k_ = (bi * 12 + ct) % 4
                for c in range(NDC):
                    T("matmul", [hs, wB], [pp[k_]], out=pp[k_][:, 0:TBL], lhsT=wB[:, c, ct * P:(ct + 1) * P],
                      rhs=hs[:, c, :], start=(c == 0), stop=(c == NDC - 1))
                evac(ct, ob[k_][:, :], pp[k_][:, 0:TBL], [pp[k_]], [ob[k_]])
                K.dma("sync", PT.t.ap()[ct * P:(ct + 1) * P, bi * TBL:(bi + 1) * TBL], ob[k_][:, :], r=[ob[k_]], w=[],
                      chan=ob[k_])
        K.end()

    def load_rowp(names):
        idx = dict(k_k=0, k_a=1, r_k=2, lnx_g=3, lnx_b=4, w0f=5, w0b=6, a0f=7, a0b=8)
        out = {}
        for n in names:
            b = K.sb("rp_" + n, [P, 256], F32)
            K.dma("gpsimd", b[:, :], I("rowp_in").ap()[idx[n]:idx[n] + 1, :].partition_broadcast(P), w=[b])
            out[n] = b
        return out

    def bc4(ap_p4):
        return ap_p4[:, :, None].broadcast_to([P, 4, 64])

    def phase2b_prep(g0, S_):
        K.begin("pr%d" % g0)
        identf = load_ident(F32)
        rp = load_rowp(["k_k", "k_a", "r_k", "w0f", "w0b", "a0f", "a0b"])
        wl = K.sb("wl", [P, 4, 256], F32)
        K.dma("gpsimd", wl[:, :, :], I("wlora_in").ap().rearrange("a r c -> r a c"), w=[wl])
        mu = K.sb("mu", [P, 2, 10], F32)
        K.dma("gpsimd", mu[:, :, :], I("mu_in").ap()[:, :, :], w=[mu])
        c0 = K.sb("c0", [P, 10], F32)
        V("tensor_tensor", [mu], [c0], out=c0[:, :], in0=mu[:, 0, :], in1=mu[:, 1, :], op=ALU.add)
        V("tensor_scalar", [c0], [c0], out=c0[:, :], in0=c0[:, :], scalar1=-1.0, scalar2=1.0, op0=ALU.mult, op1=ALU.add)
        pt_ = [K.sb("pt%d" % i, [P, 10, 130], F32) for i in range(2)]
        sl = K.sb("sl", [P, 10, P], F32)
        s2 = K.sb("s2", [P, 10, P], F32)
        tw = K.sb("tw", [P, 2, P], F32)
        pl = [K.ps("pl%d" % i, [P, 512], F32) for i in range(2)]
        ptr = [K.ps("ptr%d" % i, [P, 4, P], F32) for i in range(2)]
        stg = [K.sb("stg%d" % i, [P, 10, 256], F32) for i in range(2)]
        kk = K.sb("kk", [P, 256], F32)
        kcp = K.sb("kcp", [P, 256], F32)
        sq = K.sb("sq", [P, 256], F32)
        ss4 = K.sb("ss4", [P, 4], F32)
        aa = K.sb("aa", [P, 2, 256], F32)
        t1 = K.sb("t1", [P, 256], F32)
        t2 = K.sb("t2", [P, 256], F32)
        bw = K.sb("bw", [P, 4], F32)
        nch = S_ // P
        for ci in range(nch):
            t0 = ci * P
            pz = pt_[ci % 2]
            st = stg[ci % 2]
            lo = max(t0 - 1, 0)
            hi = min(t0 + P + 1, S_)
            do = lo - (t0 - 1)
            if do > 0:
                V("memset", [], [pz], args=(pz[:, :, 0:1], 0.0))
            if hi < t0 + P + 1:
                V("memset", [], [pz], args=(pz[:, :, 129:130], 0.0))
            K.dma("sync", pz[:, :, do:do + (hi - lo)],
                  PT.t.ap()[0:1280, lo:hi].rearrange("(c p) t -> p c t", p=P), w=[pz])
            V("tensor_tensor", [pz, c0], [sl], out=sl[:, :, :], in0=pz[:, :, 1:129],
              in1=c0[:, :, None].broadcast_to([P, 10, P]), op=ALU.mult)
            V("tensor_tensor", [pz, mu], [s2], out=s2[:, :, :], in0=pz[:, :, 0:128],
              in1=mu[:, 0, :, None].broadcast_to([P, 10, P]), op=ALU.mult)
            V("tensor_tensor", [sl, s2], [sl], out=sl[:, :, :], in0=sl[:, :, :], in1=s2[:, :, :], op=ALU.add)
            V("tensor_tensor", [pz, mu], [s2], out=s2[:, :, :], in0=pz[:, :, 2:130],
              in1=mu[:, 1, :, None].broadcast_to([P, 10, P]), op=ALU.mult)
            V("tensor_tensor", [sl, s2], [sl], out=sl[:, :, :], in0=sl[:, :, :], in1=s2[:, :, :], op=ALU.add)
            S("activation", [sl], [tw], out=tw[:, :, :], in_=sl[:, 6:8, :], func=AF.Tanh)
            for d in range(2):
                T("matmul", [tw, wl], [pl[0]], out=pl[0][:, d * 256:(d + 1) * 256], lhsT=tw[:, d, :], rhs=wl[:, d, :],
                  start=True, stop=True)
            for d in range(2):
                T("matmul", [sl, wl], [pl[1]], out=pl[1][:, d * 256:(d + 1) * 256], lhsT=sl[:, 8 + d, :], rhs=wl[:, 2 + d, :],
                  start=True, stop=True)
            for j in range(4):
                T("transpose", [sl, identf], [ptr[0]], out=ptr[0][:, j, :], in_=sl[:, j, :], identity=identf[:, :])
            for j in range(2):
                T("transpose", [sl, identf], [ptr[1]], out=ptr[1][:, j, :], in_=sl[:, 4 + j, :], identity=identf[:, :])
            S("copy", [ptr[0]], [st], out=st[:, 0, :].rearrange("p (a t) -> p a t", a=2), in_=ptr[0][:, 0:2, :])
            S("copy", [ptr[0]], [kcp], out=kcp[:, :].rearrange("p (a t) -> p a t", a=2), in_=ptr[0][:, 2:4, :])
            S("copy", [ptr[1]], [st], out=st[:, 1, :].rearrange("p (a t) -> p a t", a=2), in_=ptr[1][:, 0:2, :])
            for d in range(2):
                V("tensor_tensor", [pl[0], rp["w0f" if d == 0 else "w0b"]], [st], out=st[:, 3 + 3 * d, :],
                  in0=pl[0][:, d * 256:(d + 1) * 256], in1=rp["w0f" if d == 0 else "w0b"][:, :], op=ALU.add)
                S("activation", [st], [st], out=st[:, 3 + 3 * d, :], in_=st[:, 3 + 3 * d, :], func=AF.Sigmoid)
                V("tensor_tensor", [pl[1], rp["a0f" if d == 0 else "a0b"]], [aa], out=aa[:, d, :],
                  in0=pl[1][:, d * 256:(d + 1) * 256], in1=rp["a0f" if d == 0 else "a0b"][:, :], op=ALU.add)
            S("activation", [aa], [aa], out=aa[:, :, :], in_=aa[:, :, :], func=AF.Sigmoid)
            V("tensor_tensor", [kcp, rp["k_k"]], [kk], out=kk[:, :], in0=kcp[:, :], in1=rp["k_k"][:, :], op=ALU.mult)
            V("tensor_tensor", [kk], [sq], out=sq[:, :], in0=kk[:, :], in1=kk[:, :], op=ALU.mult)
            V("tensor_reduce", [sq], [ss4], out=ss4[:, :], in_=sq[:, :].rearrange("p (h j) -> p h j", h=4), axis=AX.X, op=ALU.add)
            S("activation", [ss4], [ss4], out=ss4[:, :], in_=ss4[:, :], func=AF.Sqrt)
            V("tensor_scalar", [ss4], [ss4], out=ss4[:, :], in0=ss4[:, :], scalar1=1e-12, scalar2=None, op0=ALU.max)
            V("reciprocal", [ss4], [ss4], out=ss4[:, :], in_=ss4[:, :])
            V("tensor_tensor", [kk, ss4], [st], out=st[:, 2, :].rearrange("p (h j) -> p h j", h=4),
              in0=kk[:, :].rearrange("p (h j) -> p h j", h=4), in1=bc4(ss4), op=ALU.mult)
            for d in range(2):
                V("scalar_tensor_tensor", [aa, rp["k_a"]], [t1], out=t1[:, :], in0=aa[:, d, :], scalar=-1.0,
                  in1=rp["k_a"][:, :], op0=ALU.add, op1=ALU.mult)
                V("scalar_tensor_tensor", [t1, kcp], [st], out=st[:, 4 + 3 * d, :], in0=t1[:, :], scalar=1.0,
                  in1=kcp[:, :], op0=ALU.add, op1=ALU.mult)
                V("tensor_tensor", [st, aa], [st], out=st[:, 5 + 3 * d, :], in0=st[:, 2, :], in1=aa[:, d, :], op=ALU.mult)
            V("tensor_tensor", [st], [t2], out=t2[:, :], in0=st[:, 4, :], in1=st[:, 7, :], op=ALU.add)
            V("tensor_tensor", [t2, st], [t2], out=t2[:, :], in0=t2[:, :], in1=st[:, 0, :], op=ALU.mult)
            V("tensor_tensor", [t2, rp["r_k"]], [t2], out=t2[:, :], in0=t2[:, :], in1=rp["r_k"][:, :], op=ALU.mult)
            V("tensor_reduce", [t2], [bw], out=bw[:, :], in_=t2[:, :].rearrange("p (h j) -> p h j", h=4), axis=AX.X, op=ALU.add)
            V("tensor_tensor", [st, bw], [st], out=st[:, 9, :].rearrange("p (h j) -> p h j", h=4),
              in0=st[:, 1, :].rearrange("p (h j) -> p h j", h=4), in1=bc4(bw), op=ALU.mult)
            K.dma("gpsimd", PREP.t.ap()[t0:t0 + P, :, :], st[:, :, :], r=[st], w=[], chan=st)
        K.end()

    def phase2b_scan(g0, S_):
        K.begin("sc%d" % g0)
        identf = load_ident(F32)
        mk = K.sb("mk", [P, 4, P], F32)
        K.dma("gpsimd", mk[:, :, :], I("masks_in").ap().rearrange("a p f -> p a f"), w=[mk])
        ones = K.sb("ones", [P, 1], F32)
        V("memset", [], [ones], args=(ones[:, :], 1.0))
        M_LT, M_GT, M_LE, M_GE = 0, 1, 2, 3
        nch = S_ // P
        NB = 8
        banks = [K.ps("bk%d" % i, [P, 512], F32) for i in range(NB)]
        bctr = [0]

        def bank():
            b = banks[bctr[0] % NB]
            bctr[0] += 1
            return b

        class St:
            pass
        sts = []
        for d in range(2):
            s = St()
            s.d = d
            s.H = K.sb("H%d" % d, [P, 4, 64], F32)
            V("memset", [], [s.H], args=(s.H[:, :, :], 0.0))
            s.inp = [K.sb("in%d_%d" % (d, i), [P, 10, 256], F32) for i in range(2)]
            for nm, shp in (("cs", [P, 512]), ("lam", [P, 4, 256]), ("kt_", [P, 256]), ("rt_", [P, 256]), ("bb_", [P, 256]),
                            ("kb_", [P, 256]), ("bh_", [P, 256]), ("kh_", [P, 256]), ("lc", [P, 2]),
                            ("trA", [P, 4, P]), ("trBm0", [P, 4, P]), ("trBm1", [P, 4, P]),
                            ("TT", [P, 4, P]), ("AkT", [P, 4, P]), ("BrbT", [P, 4, P]), ("BrkT", [P, 4, P]),
                            ("nZ", [P, 4, 64]), ("U", [P, 4, 64]), ("Y", [P, 256])):
                setattr(s, nm, K.sb("%s%d" % (nm, d), shp, F32))
            for nm in ("X", "XT", "X2", "XT2", "TTb"):
                setattr(s, nm, K.sb("%s%d" % (nm, d), [P, 4, P], BF16))
            V("memset", [], [s.trBm0], args=(s.trBm0[:, :, :], 0.0))
            V("memset", [], [s.trBm1], args=(s.trBm1[:, :, :], 0.0))
            sts.append(s)

        def mask4(i):
            return mk[:, i:i + 1, :].broadcast_to([P, 4, P])

        def hv(buf, h):
            raise NotImplementedError

        def chunk_stages(s, ci):
            d = s.d
            t0 = ci * P
            xin = s.inp[ci % 2] if d == 0 else s.inp[(nch - 1 - ci) % 2]
            SG = xin[:, 3 + 3 * d, :]
            KD = xin[:, 4 + 3 * d, :]
            BBv = xin[:, 5 + 3 * d, :]
            Rv = xin[:, 0, :]
            Vv = xin[:, 1, :]
            KAP = xin[:, 2, :]
            m_cum = M_GE if d == 0 else M_LE
            m_suf = M_LT if d == 0 else M_GT
            m_A = M_LT if d == 0 else M_GT
            m_AT = M_GT if d == 0 else M_LT
            m_BT = M_GE if d == 0 else M_LE
            stages = []

            def st_load():
                K.dma("sync" if d == 0 else "gpsimd", xin[:, :, :], PREP.t.ap()[t0:t0 + P, :, :], w=[xin])
            stages.append(st_load)

            def st_cum():
                b = bank()
                T("matmul", [mk, xin], [b], out=b[:, 0:256], lhsT=mk[:, m_cum, :], rhs=SG, start=True, stop=True)
                T("matmul", [mk, xin], [b], out=b[:, 256:512], lhsT=mk[:, m_suf, :], rhs=SG, start=True, stop=True)
                S("copy", [b], [s.cs], out=s.cs[:, :], in_=b[:, :])
                b2 = bank()
                for hf in range(2):
                    T("matmul", [xin, ones], [b2], out=b2[:, hf:hf + 1], lhsT=SG[:, hf * P:(hf + 1) * P], rhs=ones[:, :],
                      start=True, stop=True)
                S("activation", [b2], [s.lc], out=s.lc[:, :], in_=b2[:, 0:2], func=AF.Exp, scale=-DECAY_C)
                S("activation", [s.cs], [s.lam], out=s.lam[:, 0, :], in_=s.cs[:, 0:256], func=AF.Exp, scale=-DECAY_C)
                S("activation", [s.cs], [s.lam], out=s.lam[:, 2, :], in_=s.cs[:, 0:256], func=AF.Exp, scale=DECAY_C)
                S("activation", [s.cs], [s.lam], out=s.lam[:, 3, :], in_=s.cs[:, 256:512], func=AF.Exp, scale=-DECAY_C)
                V("tensor_tensor", [s.cs, xin], [s.cs], out=s.cs[:, 0:256], in0=s.cs[:, 0:256], in1=SG, op=ALU.subtract)
                S("activation", [s.cs], [s.lam], out=s.lam[:, 1, :], in_=s.cs[:, 0:256], func=AF.Exp, scale=-DECAY_C)
                V("tensor_tensor", [xin, s.lam], [s.kt_], out=s.kt_[:, :], in0=KAP, in1=s.lam[:, 1, :], op=ALU.mult)
                V("tensor_tensor", [xin, s.lam], [s.rt_], out=s.rt_[:, :], in0=Rv, in1=s.lam[:, 0, :], op=ALU.mult)
                V("tensor_tensor", [xin, s.lam], [s.bb_], out=s.bb_[:, :], in0=BBv, in1=s.lam[:, 2, :], op=ALU.mult)
                V("tensor_tensor", [xin, s.lam], [s.kb_], out=s.kb_[:, :], in0=KD, in1=s.lam[:, 2, :], op=ALU.mult)
                V("tensor_tensor", [xin, s.lam], [s.bh_], out=s.bh_[:, :], in0=BBv, in1=s.lam[:, 3, :], op=ALU.mult)
                V("tensor_tensor", [xin, s.lam], [s.kh_], out=s.kh_[:, :], in0=KD, in1=s.lam[:, 3, :], op=ALU.mult)
            stages.append(st_cum)

            def st_tr():
                b = bank()
                for q in range(4):
                    src = (s.kt_, s.rt_)[q // 2]
                    T("transpose", [src, identf], [b], out=b[:, q * P:(q + 1) * P], in_=src[:, (q % 2) * P:(q % 2 + 1) * P],
                      identity=identf[:, :])
                S("copy", [b], [s.trA], out=s.trA[:, :, :], in_=b[:, :].rearrange("p (q t) -> p q t", q=4))
                b = bank()
                for q in range(4):
                    src = (s.bb_, s.kb_)[q // 2]
                    T("transpose", [src, identf], [b], out=b[:, q * P:(q + 1) * P], in_=src[:, (q % 2) * P:(q % 2 + 1) * P],
                      identity=identf[:, :])
                V("tensor_copy", [b], [s.trBm0], out=s.trBm0[0:64, :, :], in_=b[0:64, :].rearrange("p (q t) -> p q t", q=4))
                V("tensor_copy", [b], [s.trBm1], out=s.trBm1[64:128, :, :], in_=b[64:128, :].rearrange("p (q t) -> p q t", q=4))
            stages.append(st_tr)

            def fa(q0, h):
                return s.trA[:, q0 + h // 2, :]

            def fb(q0, h):
                return (s.trBm0, s.trBm1)[h % 2][:, q0 + h // 2, :]

            def st_A():
                b = bank()
                for h in range(4):
                    T("matmul", [s.trA, s.trBm0, s.trBm1], [b], out=b[:, h * P:(h + 1) * P], lhsT=fa(0, h), rhs=fb(0, h),
                      start=True, stop=True)
                V("tensor_tensor", [b, mk], [s.X], out=s.X[:, :, :], in0=b[:, :].rearrange("p (q t) -> p q t", q=4),
                  in1=mask4(m_A), op=ALU.mult)
                b = bank()
                for h in range(4):
                    T("matmul", [s.trA, s.trBm0, s.trBm1], [b], out=b[:, h * P:(h + 1) * P], lhsT=fb(0, h), rhs=fa(0, h),
                      start=True, stop=True)
                V("tensor_tensor", [b, mk], [s.XT], out=s.XT[:, :, :], in0=b[:, :].rearrange("p (q t) -> p q t", q=4),
                  in1=mask4(m_AT), op=ALU.mult)
                V("scalar_tensor_tensor", [s.XT, identf], [s.TT], out=s.TT[:, :, :], in0=s.XT[:, :, :], scalar=-1.0,
                  in1=identf[:, None, :].broadcast_to([P, 4, P]), op0=ALU.mult, op1=ALU.add)
                S("copy", [s.TT], [s.TTb], out=s.TTb[:, :, :], in_=s.TT[:, :, :])
            stages.append(st_A)

            def mk_level(k):
                def st_level():
                    Xc, XTc = (s.X, s.XT) if k % 2 == 0 else (s.X2, s.XT2)
                    Xn, XTn = (s.X2, s.XT2) if k % 2 == 0 else (s.X, s.XT)
                    b = bank()
                    for h in range(4):
                        T("matmul", [Xc, XTc], [b], out=b[:, h * P:(h + 1) * P], lhsT=XTc[:, h, :], rhs=Xc[:, h, :],
                          start=True, stop=True)
                    S("copy", [b], [Xn], out=Xn[:, :, :], in_=b[:, :].rearrange("p (q t) -> p q t", q=4))
                    if k < 5:
                        b = bank()
                        for h in range(4):
                            T("matmul", [Xc, XTc], [b], out=b[:, h * P:(h + 1) * P], lhsT=Xc[:, h, :], rhs=XTc[:, h, :],
                              start=True, stop=True)
                        V("tensor_copy", [b], [XTn], out=XTn[:, :, :], in_=b[:, :].rearrange("p (q t) -> p q t", q=4))
                    b = bank()
                    for h in range(4):
                        T("matmul", [Xn, s.TTb], [b], out=b[:, h * P:(h + 1) * P], lhsT=Xn[:, h, :], rhs=s.TTb[:, h, :],
                          start=True, stop=True)
                    V("tensor_tensor", [b, s.TT], [s.TT], out=s.TT[:, :, :], in0=b[:, :].rearrange("p (q t) -> p q t", q=4),
                      in1=s.TT[:, :, :], op=ALU.add)
                    if k < 5:
                        S("copy", [s.TT], [s.TTb], out=s.TTb[:, :, :], in_=s.TT[:, :, :])
                return st_level
            for k in range(6):
                stages.append(mk_level(k))

            def st_B():
                for (dst, lq, rq, mi) in ((s.AkT, 2, 0, m_AT), (s.BrbT, 0, 2, m_BT), (s.BrkT, 2, 2, m_BT)):
                    b = bank()
                    for h in range(4):
                        T("matmul", [s.trA, s.trBm0, s.trBm1], [b], out=b[:, h * P:(h + 1) * P], lhsT=fb(lq, h), rhs=fa(rq, h),
                          start=True, stop=True)
                    V("tensor_tensor", [b, mk], [dst], out=dst[:, :, :], in0=b[:, :].rearrange("p (q t) -> p q t", q=4),
                      in1=mask4(mi), op=ALU.mult)
            stages.append(st_B)

            def st_seq():
                b = bank()
                for h in range(4):
                    pb = (h % 2) * 64
                    T("matmul", [s.trA, s.H], [b], out=b[:, h * 64:(h + 1) * 64], lhsT=fa(0, h), rhs=s.H[:, h, :],
                      start=True, stop=False)
                    T("matmul", [s.AkT, xin], [b], out=b[:, h * 64:(h + 1) * 64], lhsT=s.AkT[:, h, :], rhs=Vv[:, h * 64:(h + 1) * 64],
                      start=False, stop=True)
                S("activation", [b], [s.nZ], out=s.nZ[:, :, :], in_=b[:, 0:256].rearrange("p (h i) -> p h i", h=4),
                  func=AF.Copy, scale=-1.0)
                b = bank()
                for h in range(4):
                    T("matmul", [s.TT, s.nZ], [b], out=b[:, h * 64:(h + 1) * 64], lhsT=s.TT[:, h, :], rhs=s.nZ[:, h, :],
                      start=True, stop=True)
                V("tensor_copy", [b], [s.U], out=s.U[:, :, :], in_=b[:, 0:256].rearrange("p (h i) -> p h i", h=4))
                b = bank()
                for h in range(4):
                    pb = (h % 2) * 64
                    T("matmul", [s.trA, s.H], [b], out=b[:, h * 64:(h + 1) * 64], lhsT=fa(2, h), rhs=s.H[:, h, :],
                      start=True, stop=False)
                    T("matmul", [s.BrbT, s.U], [b], out=b[:, h * 64:(h + 1) * 64], lhsT=s.BrbT[:, h, :], rhs=s.U[:, h, :],
                      start=False, stop=False)
                    T("matmul", [s.BrkT, xin], [b], out=b[:, h * 64:(h + 1) * 64], lhsT=s.BrkT[:, h, :],
                      rhs=Vv[:, h * 64:(h + 1) * 64], start=False, stop=True)
                S("copy", [b], [s.Y], out=s.Y[:, :], in_=b[:, 0:256])
                K.dma("sync", YD[d].t.ap()[t0:t0 + P, :], s.Y[:, :], r=[s.Y], w=[], chan=s.Y)
                b = bank()
                for h in range(4):
                    hf = h // 2
                    T("matmul", [s.bh_, s.U], [b], out=b[:, h * 64:(h + 1) * 64], lhsT=s.bh_[:, hf * P:(hf + 1) * P], rhs=s.U[:, h, :],
                      start=True, stop=False)
                    T("matmul", [s.kh_, xin], [b], out=b[:, h * 64:(h + 1) * 64], lhsT=s.kh_[:, hf * P:(hf + 1) * P],
                      rhs=Vv[:, h * 64:(h + 1) * 64], start=False, stop=True)
                for h in range(4):
                    pb = (h % 2) * 64
                    hf = h // 2
                    V("scalar_tensor_tensor", [s.H, s.lc, b], [s.H], out=s.H[pb:pb + 64, h, :], in0=s.H[pb:pb + 64, h, :],
                      scalar=s.lc[pb:pb + 64, hf:hf + 1], in1=b[pb:pb + 64, h * 64:(h + 1) * 64], op0=ALU.mult, op1=ALU.add)
            stages.append(st_seq)
            return stages

        for n in range(nch):
            lists = [chunk_stages(sts[0], n), chunk_stages(sts[1], nch - 1 - n)]
            for i in range(len(lists[0])):
                lists[0][i]()
                lists[1][i]()
        K.end()

    lnx_parts = None

    def phase2b_post(g0, S_):
        K.begin("po%d" % g0)
        identf = load_ident(F32)
        rp = load_rowp(["lnx_g", "lnx_b"])
        yf = [K.sb("yf%d" % i, [P, 256], F32) for i in range(2)]
        yb_ = [K.sb("yb%d" % i, [P, 256], F32) for i in range(2)]
        bon = [K.sb("bon%d" % i, [P, 256], F32) for i in range(2)]
        zt = [K.sb("zt%d" % i, [P, 2, P], F32) for i in range(2)]
        s4 = K.sb("s4", [P, 4], F32)
        yc = K.sb("yc", [P, 256], F32)
        sq = K.sb("sq", [P, 256], F32)
        ptr = [K.ps("ptr%d" % i, [P, 4, P], F32) for i in range(2)]
        og = [K.sb("og%d" % i, [P, 2, P], BF16) for i in range(2)]
        for ci in range(S_ // P):
            t0 = ci * P
            u = ci % 2
            K.dma("sync", yf[u][:, :], YD[0].t.ap()[t0:t0 + P, :], w=[yf[u]])
            K.dma("sync", yb_[u][:, :], YD[1].t.ap()[t0:t0 + P, :], w=[yb_[u]])
            K.dma("gpsimd", bon[u][:, :], PREP.t.ap()[t0:t0 + P, 9, :], w=[bon[u]])
            K.dma("gpsimd", zt[u][:, :, :], PT.t.ap()[1280:1536, t0:t0 + P].rearrange("(c p) t -> p c t", p=P), w=[zt[u]])
            V("tensor_tensor", [yf[u], yb_[u]], [yf[u]], out=yf[u][:, :], in0=yf[u][:, :], in1=yb_[u][:, :], op=ALU.add)
            y4 = yf[u][:, :].rearrange("p (h j) -> p h j", h=4)
            V("tensor_reduce", [yf[u]], [s4], out=s4[:, :], in_=y4, axis=AX.X, op=ALU.add)
            V("tensor_scalar", [s4], [s4], out=s4[:, :], in0=s4[:, :], scalar1=-1.0 / 64, scalar2=None, op0=ALU.mult)
            V("tensor_tensor", [yf[u], s4], [yc], out=yc[:, :].rearrange("p (h j) -> p h j", h=4), in0=y4, in1=bc4(s4), op=ALU.add)
            V("tensor_tensor", [yc], [sq], out=sq[:, :], in0=yc[:, :], in1=yc[:, :], op=ALU.mult)
            V("tensor_reduce", [sq], [s4], out=s4[:, :], in_=sq[:, :].rearrange("p (h j) -> p h j", h=4), axis=AX.X, op=ALU.add)
            S("activation", [s4], [s4], out=s4[:, :], in_=s4[:, :], func=AF.Sqrt, scale=1.0 / 64, bias=LNX_EPS)
            V("reciprocal", [s4], [s4], out=s4[:, :], in_=s4[:, :])
            V("tensor_tensor", [yc, s4], [yc], out=yc[:, :].rearrange("p (h j) -> p h j", h=4),
              in0=yc[:, :].rearrange("p (h j) -> p h j", h=4), in1=bc4(s4), op=ALU.mult)
            V("tensor_tensor", [yc, rp["lnx_g"]], [yc], out=yc[:, :], in0=yc[:, :], in1=rp["lnx_g"][:, :], op=ALU.mult)
            V("tensor_tensor", [yc, rp["lnx_b"]], [yc], out=yc[:, :], in0=yc[:, :], in1=rp["lnx_b"][:, :], op=ALU.add)
            V("tensor_tensor", [yc, bon[u]], [yc], out=yc[:, :], in0=yc[:, :], in1=bon[u][:, :], op=ALU.add)
            S("activation", [zt[u]], [zt[u]], out=zt[u][:, :, :], in_=zt[u][:, :, :], func=AF.Silu)
            for j in range(2):
                T("transpose", [yc, identf], [ptr[u]], out=ptr[u][:, j, :], in_=yc[:, j * P:(j + 1) * P], identity=identf[:, :])
            V("tensor_tensor", [ptr[u], zt[u]], [og[u]], out=og[u][:, :, :], in0=ptr[u][:, 0:2, :], in1=zt[u][:, :, :], op=ALU.mult)
            K.dma("sync", ygT_loc.t.ap()[256:512, g0 + t0:g0 + t0 + P].rearrange("(c p) t -> p c t", p=P), og[u][:, :, :],
                  r=[og[u]], w=[], chan=og[u])
        K.end()

    ygT_all = dscr("ygT_all", [NCORES * 512, NTOK], BF16)
    ygT_own = dscr("ygT_own", [NCORES * 512, TPC], BF16)
    mT = dscr("mT", [D, TPC], BF16)
    PRE = dscr("PRE", [TPC, D], F32)
    SSQ = dscr("SSQ", [TPC, 8], F32)
    INSPEC.update({"wG": ("wG", [64, P, NDC, P]), "wOA": ("wOA", [32, P, 16, P]), "wOB": ("wOB", [32, P, 16, P]),
                   "wOUT": ("wOUT", [8, P, NDC, 512]), "final_g": ("final_g", [1, D])})

    def phase_ag2():
        K.begin("ag2")
        K.allgather(ygT_loc, ygT_all)
        K.end()
        K.begin("own")
        rk = {}

        def own_src(h, r8):
            if "r" not in rk:
                rk["r"] = h.partition_id() % NCORES
            return ygT_all.t.ap()[r8 * 512:(r8 + 1) * 512, bass.ds(rk["r"] * TPC, TPC)]
        for r8 in range(NCORES):
            K.dma("gpsimd", ygT_own.t.ap()[r8 * 512:(r8 + 1) * 512, :], Lazy(lambda h, r8=r8: own_src(h, r8)),
                  r=[], w=[], chan=ygT_own)
        K.end()

    def phase3a():
        K.begin("p3a")
        TB = min(512, TPC)
        hTb = [K.sb("hTb%d" % i, [P, NDC, TB], BF16) for i in range(1)]
        ygb = [K.sb("ygb%d" % i, [P, 32, TB], BF16) for i in range(1)]
        wg = [K.sb("wg%d" % i, [P, 2, NDC, P], BF16) for i in range(2)]
        wo = [K.sb("wo%d" % i, [P, 2, 16, P], BF16) for i in range(2)]
        pg = [[K.ps("pg%d_%d" % (i, j), [P, 512], F32) for j in range(4)] for i in range(2)]
        sa = K.sb("sa", [P, TB], F32)
        sb_ = K.sb("sb", [P, TB], F32)
        m1 = K.sb("m1", [P, TB], F32)
        m2 = K.sb("m2", [P, TB], F32)
        mo = [K.sb("mo%d" % i, [P, TB], BF16) for i in range(2)]
        hv = hT_loc.t.ap().rearrange("(c p) t -> p c t", p=P)
        it = 0
        for bi in range(TPC // TB):
            hs = hTb[0]
            ys = ygb[0]
            for q in range(4):
                K.dma("sync", hs[:, q * 8:(q + 1) * 8, :], hv[:, q * 8:(q + 1) * 8, bi * TB:(bi + 1) * TB], w=[hs], join=True)
            yv = ygT_own.t.ap().rearrange("(q p) t -> p q t", p=P)[:, :, bi * TB:(bi + 1) * TB]
            for q in range(4):
                K.dma("sync", ys[:, q * 8:(q + 1) * 8, :], yv[:, q * 8:(q + 1) * 8, :], w=[ys], join=True)
            for f in range(NDC):
                u = it % 2
                it += 1
                K.dma("gpsimd", wg[u][:, 0, :, :], I("wG").ap()[f, :, :, :], w=[wg[u]], join=True)
                K.dma("gpsimd", wg[u][:, 1, :, :], I("wG").ap()[32 + f, :, :, :], w=[wg[u]], join=True)
                K.dma("gpsimd", wo[u][:, 0, :, :], I("wOA").ap()[f, :, :, :], w=[wo[u]], join=True)
                K.dma("gpsimd", wo[u][:, 1, :, :], I("wOB").ap()[f, :, :, :], w=[wo[u]], join=True)
                ga, gb, oa, ob = pg[u]
                for c in range(NDC):
                    T("matmul", [wg[u], hs], [ga], out=ga[:, 0:TB], lhsT=wg[u][:, 0, c, :], rhs=hs[:, c, :], start=(c == 0),
                      stop=(c == NDC - 1))
                for c in range(NDC):
                    T("matmul", [wg[u], hs], [gb], out=gb[:, 0:TB], lhsT=wg[u][:, 1, c, :], rhs=hs[:, c, :], start=(c == 0),
                      stop=(c == NDC - 1))
                for k in range(16):
                    q = (k // 2) * 4 + (k % 2)
                    T("matmul", [wo[u], ys], [oa], out=oa[:, 0:TB], lhsT=wo[u][:, 0, k, :], rhs=ys[:, q, :], start=(k == 0),
                      stop=(k == 15))
                for k in range(16):
                    q = (k // 2) * 4 + 2 + (k % 2)
                    T("matmul", [wo[u], ys], [ob], out=ob[:, 0:TB], lhsT=wo[u][:, 1, k, :], rhs=ys[:, q, :], start=(k == 0),
                      stop=(k == 15))
                S("activation", [ga], [sa], out=sa[:, :], in_=ga[:, 0:TB], func=AF.Sigmoid)
                S("activation", [gb], [sb_], out=sb_[:, :], in_=gb[:, 0:TB], func=AF.Sigmoid)
                V("tensor_tensor", [oa, sa], [m1], out=m1[:, :], in0=oa[:, 0:TB], in1=sa[:, :], op=ALU.mult)
                V("tensor_tensor", [ob, sb_], [m2], out=m2[:, :], in0=ob[:, 0:TB], in1=sb_[:, :], op=ALU.mult)
                V("tensor_tensor", [m1, m2], [mo[u]], out=mo[u][:, :], in0=m1[:, :], in1=m2[:, :], op=ALU.add)
                K.dma("sync", mT.t.ap()[f * P:(f + 1) * P, bi * TB:(bi + 1) * TB], mo[u][:, :], r=[mo[u]], w=[], chan=mo[u])
        K.end()

    def phase3b():
        K.begin("p3b")
        TQ = min(1024, TPC)
        mq = K.sb("mq", [P, NDC, TQ], BF16)
        wo = [K.sb("wo%d" % i, [P, NDC, 512], BF16) for i in range(2)]
        po = [K.ps("po%d" % i, [P, 512], F32) for i in range(4)]
        xt = [K.sb("xt%d" % i, [P, 512], F32) for i in range(3)]
        pr = [K.sb("pr%d" % i, [P, 512], F32) for i in range(3)]
        jk = K.sb("jk", [P, 512], BF16)
        ssq = K.sb("ssq", [P, TQ // P, 8], F32)
        mv = mT.t.ap().rearrange("(c p) t -> p c t", p=P)
        it = 0
        for tq in range(TPC // TQ):
            for q in range(4):
                K.dma("sync", mq[:, q * 8:(q + 1) * 8, :], mv[:, q * 8:(q + 1) * 8, tq * TQ:(tq + 1) * TQ], w=[mq], join=True)
            for cg in range(8):
                ws = wo[cg % 2]
                for q in range(4):
                    K.dma("gpsimd", ws[:, q * 8:(q + 1) * 8, :], I("wOUT").ap()[cg, :, q * 8:(q + 1) * 8, :], w=[ws], join=True)
                for ti in range(TQ // P):
                    u = it % 3
                    pb = po[it % 4]
                    it += 1
                    r0 = tq * TQ + ti * P
                    K.dma("sync", xt[u][:, :], I("x_own").ap()[r0:r0 + P, cg * 512:(cg + 1) * 512], w=[xt[u]])
                    for c in range(NDC):
                        T("matmul", [mq, ws], [pb], out=pb[:, :], lhsT=mq[:, c, ti * P:(ti + 1) * P], rhs=ws[:, c, :],
                          start=(c == 0), stop=(c == NDC - 1))
                    V("tensor_tensor", [pb, xt[u]], [pr[u]], out=pr[u][:, :], in0=pb[:, :], in1=xt[u][:, :], op=ALU.add)
                    S("activation", [pr[u]], [jk, ssq], out=jk[:, :], in_=pr[u][:, :], func=AF.Square,
                      accum_out=ssq[:, ti, cg:cg + 1])
                    K.dma("sync", PRE.t.ap()[r0:r0 + P, cg * 512:(cg + 1) * 512], pr[u][:, :], r=[pr[u]], w=[], chan=pr[u])
            K.dma("sync", SSQ.t.ap()[tq * TQ:(tq + 1) * TQ, :].rearrange("(n p) c -> p n c", p=P), ssq[:, :, :], r=[ssq], w=[],
                  chan=ssq)
        K.end()

    def phase3c(y_out):
        K.begin("p3c")
        fg = K.sb("fg", [P, D], F32)
        K.dma("gpsimd", fg[:, :], I("final_g").ap()[0:1, :].partition_broadcast(P), w=[fg])
        pr = [K.sb("pr%d" % i, [P, D], F32) for i in range(2)]
        oo = [K.sb("oo%d" % i, [P, D], F32) for i in range(2)]
        s8 = [K.sb("s8_%d" % i, [P, 8], F32) for i in range(2)]
        rs = [K.sb("rs%d" % i, [P, 1], F32) for i in range(2)]
        for i in range(NT):
            u = i % 2
            K.dma("sync", pr[u][:, :], PRE.t.ap()[i * P:(i + 1) * P, :], w=[pr[u]])
            K.dma("sync", s8[u][:, :], SSQ.t.ap()[i * P:(i + 1) * P, :], w=[s8[u]])
            V("tensor_reduce", [s8[u]], [rs[u]], out=rs[u][:, :], in_=s8[u][:, :], axis=AX.X, op=ALU.add)
            S("activation", [rs[u]], [rs[u]], out=rs[u][:, :], in_=rs[u][:, :], func=AF.Sqrt, scale=1.0 / D, bias=NORM_EPS)
            V("reciprocal", [rs[u]], [rs[u]], out=rs[u][:, :], in_=rs[u][:, :])
            V("scalar_tensor_tensor", [pr[u], rs[u], fg], [oo[u]], out=oo[u][:, :], in0=pr[u][:, :], scalar=rs[u][:, 0:1],
              in1=fg[:, :], op0=ALU.mult, op1=ALU.mult)
            K.dma("gpsimd", y_out.ap()[i * P:(i + 1) * P, :], oo[u][:, :], r=[oo[u]], w=[], chan=oo[u])
        K.end()

    y_out = None
    if debug is None or debug == "full":
        y_out = nc.dram_tensor("y_own", [TPC, D], F32, kind="ExternalOutput")
    phase1()
    if y_out is not None:
        for (g0, S_) in cfg.seqs:
            phase2a_proj(g0, S_)
            phase2a_attn(g0, S_)
            phase2b_proj(g0, S_)
            phase2b_prep(g0, S_)
            phase2b_scan(g0, S_)
            phase2b_post(g0, S_)
        phase_ag2()
        phase3a()
        phase3b()
        phase3c(y_out)
    if debug == "qkv":
        g0, S_ = cfg.seqs[0]
        phase2a_proj(g0, S_)
        K.begin("dbg")
        K.dma("gpsimd", dbg.ap()[0:128, :], QT[0].t.ap()[:, :], r=[], w=[], chan=QT[0])
        K.dma("gpsimd", dbg.ap()[128:256, :], KT[0].t.ap()[:, :], r=[], w=[], chan=KT[0])
        K.dma("gpsimd", dbg.ap()[256:256 + SMAX, 0:256], VS.t.ap()[:, :], r=[], w=[], chan=VS)
        K.end()
        phase2a_attn(g0, S_)
    if debug == "p2b":
        lvl = 4
        for (g0, S_) in cfg.seqs:
            phase2b_proj(g0, S_)
            if lvl >= 2:
                phase2b_prep(g0, S_)
            if lvl >= 3:
                phase2b_scan(g0, S_)
            if lvl >= 4:
                phase2b_post(g0, S_)
        K.begin("dbg")
        K.dma("gpsimd", dbg.ap()[:, :], ygT_loc.t.ap()[:, :], r=[], w=[], chan=ygT_loc)
        K.end()
    if debug == "p2a":
        for (g0, S_) in cfg.seqs:
            phase2a_proj(g0, S_)
            phase2a_attn(g0, S_)
        K.begin("dbg")
        K.dma("gpsimd", dbg.ap()[:, :], ygT_loc.t.ap()[:, :], r=[], w=[], chan=ygT_loc)
        K.end()
    K.stack.close()
    print("ninst", K.ninst)
    nc._used_inputs = set(dt.keys())
    return nc


def rope_tables(smax):
    inv = (1.0 / (10000.0 ** (np.arange(0, 64, 2, dtype=np.float32) / np.float32(64)))).astype(np.float32)
    ang = (np.arange(smax, dtype=np.float32)[:, None] * inv[None, :]).astype(np.float32)
    return np.cos(ang).astype(np.float32), np.sin(ang).astype(np.float32)


def host_consts(cfg):
    smax = max(s for _, s in cfg.seqs)
    c, s = rope_tables(smax)
    pi = np.arange(P)[:, None]
    fi = np.arange(P)[None, :]
    masks = np.stack([fi < pi, fi > pi, fi <= pi, fi >= pi]).astype(np.float32)
    return {"ident": np.eye(P, dtype=np.float32), "cos_t": c, "sin_t": s, "masks": masks}


def core_inputs(cfg, c, inp):
    W_A_ = D // 2
    m = {}
    xall = inp["x_all"]
    m["x_own"] = xall[c * cfg.tpc:(c + 1) * cfg.tpc]
    m["norm_g"] = inp["norm_g"].reshape(1, D)
    w_in = inp["w_in"][0]
    qs = slice(c * 256, (c + 1) * 256)
    m["w_inA"] = np.ascontiguousarray(np.concatenate(
        [w_in[:, 0 * W_A_:][:, qs], w_in[:, 1 * W_A_:][:, qs], w_in[:, 2 * W_A_:][:, qs], w_in[:, OFF_Z:][:, qs]], axis=1))
    m["lam_in"] = np.stack([inp["lam_q1"][0], inp["lam_k1"][0], inp["lam_q2"][0], inp["lam_k2"][0]]).astype(np.float32)
    m["subln_g"] = inp["subln_g"].reshape(1, 128)
    W_B_ = D // 2
    o3 = 3 * W_A_
    m["w_inB"] = np.ascontiguousarray(np.concatenate(
        [w_in[:, o3:][:, qs], w_in[:, o3 + W_B_:][:, qs], w_in[:, o3 + 2 * W_B_:][:, qs],
         w_in[:, o3 + 3 * W_B_:o3 + 3 * W_B_ + 512], w_in[:, OFF_Z + W_A_:][:, qs]], axis=1))
    def musl(v):
        return np.concatenate([v[0 * W_B_:][qs], v[1 * W_B_:][qs], v[2 * W_B_:][qs], v[3 * W_B_:3 * W_B_ + 512]])
    mu2 = np.stack([musl(inp["mu_prev"][0]), musl(inp["mu_next"][0])]).astype(np.float32)
    m["mu_in"] = np.ascontiguousarray(mu2.reshape(2, 10, P).transpose(2, 0, 1))
    m["wlora_in"] = np.ascontiguousarray(np.stack([inp["w_lora"][0, 0][:, qs], inp["w_lora"][0, 1][:, qs],
                                                   inp["a_lora"][0, 0][:, qs], inp["a_lora"][0, 1][:, qs]]))
    if c == 0 or "wG" not in inp.get("_cache", {}):
        cache = inp.setdefault("_cache", {})
        wg = w_in[:, OFF_G:OFF_G + 2 * D]
        cache["wG"] = np.ascontiguousarray(wg.reshape(NDC, P, 64, P).transpose(2, 1, 0, 3))
        cache["wOA"] = np.ascontiguousarray(inp["w_oA"][0].reshape(16, P, 32, P).transpose(2, 1, 0, 3))
        cache["wOB"] = np.ascontiguousarray(inp["w_oB"][0].reshape(16, P, 32, P).transpose(2, 1, 0, 3))
        cache["wOUT"] = np.ascontiguousarray(inp["w_out"][0].reshape(NDC, P, 8, 512).transpose(2, 1, 0, 3))
    for k_ in ("wG", "wOA", "wOB", "wOUT"):
        m[k_] = inp["_cache"][k_]
    m["final_g"] = inp["final_g"].reshape(1, D)
    m["rowp_in"] = np.ascontiguousarray(np.stack([
        inp["k_k"][0][qs], inp["k_a"][0][qs], inp["r_k"][0].reshape(-1)[qs], inp["lnx_g"][0][qs], inp["lnx_b"][0][qs],
        inp["w0"][0, 0][qs], inp["w0"][0, 1][qs], inp["a0"][0, 0][qs], inp["a0"][0, 1][qs]]))
    return m


_BUILD_CACHE = {}


def kernel(**inputs):
    inp = {k: np.asarray(v) for k, v in inputs.items()}
    xp = inp.pop("x_prompt")
    xs = inp.pop("x_sample")
    B, S, _ = xp.shape
    DS = xs.shape[1]
    cfg = CFG(batch=B, seq=S, dec_seq=DS)
    inp["x_all"] = np.concatenate([xp.reshape(-1, D), xs.reshape(-1, D)], 0)
    key = (B, S, DS)
    if key not in _BUILD_CACHE:
        _BUILD_CACHE[key] = build(cfg)
    nc = _BUILD_CACHE[key]
    hc = host_consts(cfg)
    in_maps = []
    for c in range(NCORES):
        m = core_inputs(cfg, c, inp)
        m.update(hc)
        in_maps.append({k: v for k, v in m.items() if k in nc._used_inputs})
    res = run_bass_kernel_spmd(nc, in_maps, core_ids=list(range(NCORES)))
    y = np.concatenate([np.asarray(res.results[c]["y_own"]) for c in range(NCORES)], 0)
    npr = B * S
    return (y[:npr].reshape(B, S, D).astype(np.float32), y[npr:].reshape(1, DS, D).astype(np.float32))
```
